# Optimizing a Trainium2 kernel written in Bass

```python
import jax, jax.numpy as jnp
from jax import lax
import numpy as np

D_MODEL = 1024
BATCH = 8
SEQ = 4096
DEPTH = 1

CHUNK = 64
N_LEFT_CHUNKS = 8
HEAD_DIM = 64
N_HEADS_A = 8
N_HEADS_B = 8
WIDTH_A = N_HEADS_A * HEAD_DIM
WIDTH_B = N_HEADS_B * HEAD_DIM
MIX_WIDTH = WIDTH_A + WIDTH_B
MAX_REL = 128
Q_BLOCK = 128
D_FF = 2816
CONV_WIDTH = 3
EPS = 1e-6
NEG_INF = -1e30

kernel_name = "hybrid_chunked_stickbreaking_convffn"


def rmsnorm(x, g):
    xf = x.astype(jnp.float32)
    y = xf * lax.rsqrt(jnp.mean(xf * xf, axis=-1, keepdims=True) + EPS)
    return (y * g.astype(jnp.float32)).astype(x.dtype)


def chunked_attention(q, k, v, bias_table):
    B, S, H, D = q.shape
    nc = S // CHUNK
    band = (N_LEFT_CHUNKS + 1) * CHUNK
    qc = q.reshape(B, nc, CHUNK, H, D)

    def gather_band(t):
        tc = t.reshape(B, nc, CHUNK, H, D)
        tp = jnp.pad(tc, ((0, 0), (N_LEFT_CHUNKS, 0), (0, 0), (0, 0), (0, 0)))
        return jnp.concatenate([tp[:, j:j + nc] for j in range(N_LEFT_CHUNKS + 1)], axis=2)

    kb = gather_band(k)
    vb = gather_band(v)
    scores = jnp.einsum('bnqhd,bnkhd->bnhqk', qc, kb).astype(jnp.float32) * (D ** -0.5)
    qi = jnp.arange(CHUNK)
    kj = jnp.arange(band)
    dist = N_LEFT_CHUNKS * CHUNK + qi[:, None] - kj[None, :]
    rel_idx = jnp.clip(dist, -MAX_REL, MAX_REL) + MAX_REL
    bias = bias_table[:, rel_idx].astype(jnp.float32)
    key_chunk = jnp.arange(nc)[:, None] - N_LEFT_CHUNKS + (kj // CHUNK)[None, :]
    valid = key_chunk >= 0
    scores = jnp.where(valid[None, :, None, None, :], scores + bias[None, None], NEG_INF)
    p = jax.nn.softmax(scores, axis=-1).astype(v.dtype)
    o = jnp.einsum('bnhqk,bnkhd->bnqhd', p, vb)
    return o.reshape(B, S, H * D)


def stick_breaking_attention(q, k, v):
    B, S, H, D = q.shape
    scale = D ** -0.5
    outs = []
    for blk in range(S // Q_BLOCK):
        q0 = blk * Q_BLOCK
        kv_len = q0 + Q_BLOCK
        qb = q[:, q0:kv_len]
        kb = k[:, :kv_len]
        vb = v[:, :kv_len]
        z = jnp.einsum('bqhd,bkhd->bhqk', qb, kb).astype(jnp.float32) * scale
        t_idx = q0 + jnp.arange(Q_BLOCK)[:, None]
        s_idx = jnp.arange(kv_len)[None, :]
        causal = s_idx < t_idx
        log_beta = jax.nn.log_sigmoid(z)
        log_1m = jnp.where(causal, log_beta - z, 0.0)
        suffix = lax.cumsum(log_1m, axis=log_1m.ndim - 1, reverse=True) - log_1m
        a = jnp.where(causal, jnp.exp(log_beta + suffix), 0.0).astype(v.dtype)
        outs.append(jnp.einsum('bhqk,bkhd->bqhd', a, vb))
    o = jnp.concatenate(outs, axis=1)
    return o.reshape(B, S, H * D)


def conv_gated_mlp(h, w_ffn_in, conv_w, conv_b, w_ffn_out):
    u = h @ w_ffn_in
    c = u.shape[-1]
    u = lax.conv_general_dilated(
        u, conv_w[:, None, :], window_strides=(1,), padding=[(CONV_WIDTH - 1, 0)],
        dimension_numbers=('NWC', 'WIO', 'NWC'), feature_group_count=c) + conv_b
    gate, val = jnp.split(u, 2, axis=-1)
    return (jax.nn.silu(gate) * val) @ w_ffn_out


def setup_inputs(seed: int = 0) -> dict:
    key = jax.random.key(seed)
    ks = jax.random.split(key, 16)
    f32 = jnp.float32
    x = jax.random.normal(ks[0], (BATCH, SEQ, D_MODEL), f32)
    norm1_g = 1.0 + 0.02 * jax.random.normal(ks[1], (DEPTH, D_MODEL), f32)
    w_in = jax.random.normal(ks[2], (DEPTH, D_MODEL, 3 * MIX_WIDTH), f32) * D_MODEL ** -0.5
    rel_bias = 0.1 * jax.random.normal(ks[3], (DEPTH, N_HEADS_A, 2 * MAX_REL + 1), f32)
    norm_a_g = 1.0 + 0.02 * jax.random.normal(ks[4], (DEPTH, WIDTH_A), f32)
    norm_b_g = 1.0 + 0.02 * jax.random.normal(ks[5], (DEPTH, WIDTH_B), f32)
    w_out = jax.random.normal(ks[6], (DEPTH, MIX_WIDTH, D_MODEL), f32) * MIX_WIDTH ** -0.5
    norm2_g = 1.0 + 0.02 * jax.random.normal(ks[7], (DEPTH, D_MODEL), f32)
    w_ffn_in = jax.random.normal(ks[8], (DEPTH, D_MODEL, 2 * D_FF), f32) * D_MODEL ** -0.5
    conv_w = jax.random.normal(ks[9], (DEPTH, CONV_WIDTH, 2 * D_FF), f32) * CONV_WIDTH ** -0.5
    conv_b = 0.01 * jax.random.normal(ks[10], (DEPTH, 2 * D_FF), f32)
    w_ffn_out = jax.random.normal(ks[11], (DEPTH, D_FF, D_MODEL), f32) * D_FF ** -0.5
    final_g = 1.0 + 0.02 * jax.random.normal(ks[12], (D_MODEL,), f32)
    return {"x": x, "norm1_g": norm1_g, "w_in": w_in, "rel_bias": rel_bias,
            "norm_a_g": norm_a_g, "norm_b_g": norm_b_g, "w_out": w_out,
            "norm2_g": norm2_g, "w_ffn_in": w_ffn_in, "conv_w": conv_w,
            "conv_b": conv_b, "w_ffn_out": w_ffn_out, "final_g": final_g}


def reference(x, norm1_g, w_in, rel_bias, norm_a_g, norm_b_g, w_out,
              norm2_g, w_ffn_in, conv_w, conv_b, w_ffn_out, final_g):
    B, S, _ = x.shape
    for l in range(DEPTH):
        h = rmsnorm(x, norm1_g[l])
        proj = h @ w_in[l]
        qa, ka, va, qb, kb, vb = jnp.split(
            proj, [WIDTH_A, 2 * WIDTH_A, 3 * WIDTH_A,
                   3 * WIDTH_A + WIDTH_B, 3 * WIDTH_A + 2 * WIDTH_B], axis=-1)
        to_heads = lambda t, nh: t.reshape(B, S, nh, HEAD_DIM)
        ya = chunked_attention(to_heads(qa, N_HEADS_A), to_heads(ka, N_HEADS_A),
                               to_heads(va, N_HEADS_A), rel_bias[l])
        yb = stick_breaking_attention(to_heads(qb, N_HEADS_B), to_heads(kb, N_HEADS_B),
                                      to_heads(vb, N_HEADS_B))
        mixed = jnp.concatenate([rmsnorm(ya, norm_a_g[l]), rmsnorm(yb, norm_b_g[l])], axis=-1)
        x = x + mixed @ w_out[l]
        x = x + conv_gated_mlp(rmsnorm(x, norm2_g[l]), w_ffn_in[l], conv_w[l], conv_b[l], w_ffn_out[l])
    return rmsnorm(x, final_g)
```

```python
import numpy as np
import concourse.bass as bass
import concourse.mybir as mybir
from concourse.bass_utils import run_bass_kernel_spmd

F32 = mybir.dt.float32
BF16 = mybir.dt.bfloat16
AF = mybir.ActivationFunctionType
ALU = mybir.AluOpType
AX = mybir.AxisListType

D = 1024
DFF = 2816
NJ = DFF // 128
EPS = 1e-6
NEG = -1e30


class Buf:
    __slots__ = ("w", "r")

    def __init__(self):
        self.w = None
        self.r = []


class Sched:
    def __init__(self, nc):
        self.nc = nc
        self.eng = {"pe": nc.tensor, "act": nc.scalar, "dve": nc.vector, "pool": nc.gpsimd, "sp": nc.sync}
        self.semobj = {k: nc.alloc_semaphore("sem_" + k) for k in self.eng}
        self.cnt = {k: 0 for k in self.eng}
        self.waited = {k: {} for k in self.eng}
        self.n_wait = 0
        self.n_inst = 0

    def dma_sem(self, name):
        self.semobj[name] = self.nc.alloc_semaphore("dsem_" + name)
        self.cnt[name] = 0
        return name

    def _needs(self, e, reads, writes):
        deps = {}

        def add(m, same_ok):
            if m is None:
                return
            k, v = m
            if k == e and not same_ok:
                return
            if deps.get(k, 0) < v:
                deps[k] = v

        raw_ok = e != "pe"
        for b in reads:
            add(b.w, raw_ok)
        for b in writes:
            add(b.w, raw_ok)
            for m in b.r:
                add(m, False)
        need = []
        for k, v in deps.items():
            if self.waited[e].get(k, 0) >= v:
                continue
            need.append((k, v))
        return need

    def _mark(self, marker, reads, writes):
        for b in reads:
            b.r.append(marker)
        for b in writes:
            b.w = marker
            b.r = []

    def op(self, e, fn, reads=(), writes=()):
        need = self._needs(e, reads, writes)
        eng = self.eng[e]
        for k, v in need[:-1]:
            eng.wait_ge(self.semobj[k], v)
            self.waited[e][k] = v
            self.n_wait += 1
        inst = fn(eng)
        if need:
            k, v = need[-1]
            inst._wait_ge(self.semobj[k], v)
            self.waited[e][k] = v
        inst.then_inc(self.semobj[e], 1)
        self.cnt[e] += 1
        self.n_inst += 1
        self._mark((e, self.cnt[e]), reads, writes)

    def dma(self, q, sem, out, in_, reads=(), writes=()):
        need = [(k, v) for (k, v) in self._needs(q, reads, writes) if k != sem]
        eng = self.eng[q]
        for k, v in need:
            eng.wait_ge(self.semobj[k], v)
            self.waited[q][k] = v
            self.n_wait += 1
        eng.dma_start(out=out, in_=in_).then_inc(self.semobj[sem], 16)
        self.cnt[sem] += 16
        self.n_inst += 1
        self._mark((sem, self.cnt[sem]), reads, writes)

    def barrier(self):
        for e, eng in self.eng.items():
            for k, c in self.cnt.items():
                if k == e or c == 0:
                    continue
                if self.waited[e].get(k, 0) >= c:
                    continue
                eng.wait_ge(self.semobj[k], c)
                self.waited[e][k] = c
                self.n_wait += 1

    def finish(self, sems):
        eng = self.eng["sp"]
        for k in sems:
            if self.cnt[k] > self.waited["sp"].get(k, 0):
                eng.wait_ge(self.semobj[k], self.cnt[k])


class Arena:
    def __init__(self, nc):
        self.nc = nc
        self.base = (nc.sbuf_base + 63) // 64 * 64
        self.top = nc.sbuf_top
        self.off = self.base
        self.n = 0

    def alloc(self, name, shape, dt):
        nb = int(np.prod(shape[1:])) * (4 if dt == F32 else 2)
        nb = (nb + 63) // 64 * 64
        assert self.off + nb <= self.top, f"SBUF overflow at {name}: {self.off + nb - self.base} > {self.top - self.base}"
        t = self.nc.alloc_sbuf_tensor_at(f"{name}_{self.n}", list(shape), dt, offset=self.off)
        self.n += 1
        self.off += nb
        return t

    def alloc_at(self, name, shape, dt, off):
        t = self.nc.alloc_sbuf_tensor_at(f"{name}_{self.n}", list(shape), dt, offset=off)
        self.n += 1
        return t

    def mark(self):
        return self.off

    def release(self, m):
        self.off = m


def build_nc(S, dbg=None):
    NT = S // 128
    NG = S // 512
    nc = bass.Bass("TRN2", target_bir_lowering=False)

    def din(name, shape):
        return nc.dram_tensor(name, list(shape), F32, kind="ExternalInput").ap()

    x_d = din("x", [S, D])
    wqk_d = din("wqk", [8, 128, 2, 8, 128])
    wv_d = din("wv", [2, 128, 8, 512])
    g1t_d = din("g1t", [128, 8])
    g2t_d = din("g2t", [128, 8])
    gfb_d = din("gfb", [128, D])
    biasT_d = din("biasT", [8, 128, 2, 128])
    c256_d = din("c256", [128, 8])
    gab_d = din("gab", [128, 8])
    wo_d = din("wo", [128, 8, D])
    wfi_d = din("wfi", [NJ, 128, 2, 8, 128])
    cw_d = din("cw", [128, 2 * NJ, 3])
    cb_d = din("cb", [128, 2 * NJ])
    wfo_d = din("wfo", [128, NJ, D])
    cst_d = din("cst", [128, 6, 128])
    out_d = nc.dram_tensor("out", [S, D], F32, kind="ExternalOutput").ap()
    dbg_d = None
    if dbg is not None:
        dbg_d = nc.dram_tensor("dbg", [128, 8 * S], BF16, kind="ExternalOutput").ap()
        dbg2_d = nc.dram_tensor("dbg2", [128, NT * 12], F32, kind="ExternalOutput").ap()

    sc = Sched(nc)
    ar = Arena(nc)
    op, dma = sc.op, sc.dma

    psall = nc.alloc_psum_tensor("psall", [128, 8 * 512], F32)
    ps = [psall[:, i * 512:(i + 1) * 512] for i in range(8)]
    psb = [Buf() for _ in range(8)]

    def psbf(i):
        return ps[i].bitcast(BF16)

    def ps2(b0):
        return psall[:, b0 * 512:(b0 + 2) * 512].rearrange("p (h c) -> p h c", h=2)

    yT = ar.alloc("yT", [128, 8, S], BF16)
    yT_b = [[Buf() for _ in range(NT)] for _ in range(8)]
    cstb = ar.alloc("cstb", [128, 4, 128], BF16)
    maskA = ar.alloc("maskA", [128, 2, 128], F32)
    c256 = ar.alloc("c256", [128, 8], F32)
    gab = ar.alloc("gab", [128, 8], F32)
    ssqA = ar.alloc("ssqA", [128, NT, 8], F32)
    ssqB = ar.alloc("ssqB", [128, NT, 4], F32)
    onesf = ar.alloc("onesf", [128, 1], F32)
    cst_b = Buf(); small_c = Buf(); ssqA_b = Buf(); ssqB_b = Buf()
    ident = cstb[:, 0, :]
    negU = cstb[:, 1, :]
    negones = cstb[:, 2, :]
    maskneg = cstb[:, 3, :]
    s_cst = sc.dma_sem("cst")
    dma("pool", s_cst, cstb[:], cst_d[:, 0:4, :], writes=[cst_b])
    dma("sp", s_cst, maskA[:], cst_d[:, 4:6, :], writes=[cst_b])
    dma("sp", s_cst, c256[:], c256_d[:], writes=[cst_b])
    dma("sp", s_cst, gab[:], gab_d[:], writes=[cst_b])
    op("pool", lambda e: e.memset(onesf[:], 1.0), writes=[small_c])
    persist_mark = ar.mark()

    hT = ar.alloc("hT", [128, 8, S], BF16)
    hT_b = [Buf() for _ in range(NT)]
    wo = ar.alloc_at("wo", [128, 8, D], BF16, persist_mark)
    wo_b = Buf()
    s_wo = sc.dma_sem("wo")
    Vbytes = NT * 8 * 65 * 2
    v_off = ar.mark()
    VA = ar.alloc("VA", [128, NT, 8, 65], BF16)
    ar.release(v_off)
    VB = ar.alloc("VB", [128, NT, 512], BF16)
    ar.release(v_off + (Vbytes + 63) // 64 * 64)
    V_b = [Buf() for _ in range(NT)]
    qT = ar.alloc("qT", [128, S], BF16)
    kT = ar.alloc("kT", [128, S], BF16)
    q_b = [Buf() for _ in range(NG)]
    k_b = [Buf() for _ in range(NG)]
    p12_mark = ar.mark()

    def rstd_ops(ss_ap, tmp_ap, out_ap, n, bufs):
        op("act", lambda e: e.activation(out=tmp_ap, in_=ss_ap, func=AF.Ln, scale=1.0 / n, bias=EPS),
           reads=bufs, writes=bufs)
        op("act", lambda e: e.activation(out=out_ap, in_=tmp_ap, func=AF.Exp, scale=-0.5),
           reads=bufs, writes=bufs)

    wv_off = ar.mark()
    wv = ar.alloc("wv", [128, 8, 512], BF16)
    wv_b = Buf()
    s_wv = sc.dma_sem("wv")
    dma("pool", s_wv, wv[:], wv_d[0], writes=[wv_b])
    op("pool", lambda e: e.memset(VA[:, :, :, 64:65], 1.0), writes=V_b)
    wv_mark = ar.mark()
    NXS = 3
    xt = [ar.alloc("xt", [128, D], F32) for _ in range(NXS)]
    xt_b = [Buf() for _ in range(NXS)]
    s_x = [sc.dma_sem(f"x{i}") for i in range(NXS)]
    g1t = ar.alloc("g1t", [128, 8], F32)
    g1_b = Buf()
    s_g = sc.dma_sem("g")
    dma("sp", s_g, g1t[:], g1t_d[:], writes=[g1_b])
    hb = [ar.alloc("hb", [128, D], BF16) for _ in range(2)]
    hb_b = [Buf() for _ in range(2)]
    junk = ar.alloc("junk", [128, D], BF16)
    junk_b = Buf()
    st1 = ar.alloc("st1", [128, 3 * NT], F32)
    st1_b = [Buf() for _ in range(NT)]

    def p1_load(i):
        sl = i % NXS
        dma("sp", s_x[sl], xt[sl][:], x_d[i * 128:(i + 1) * 128, :], writes=[xt_b[sl]])

    def p1_stats(i):
        sl = i % NXS
        op("act", lambda e: e.activation(out=junk[:], in_=xt[sl][:], func=AF.Square, accum_out=st1[:, 3 * i:3 * i + 1]),
           reads=[xt_b[sl]], writes=[junk_b, st1_b[i]])
        rstd_ops(st1[:, 3 * i:3 * i + 1], st1[:, 3 * i + 1:3 * i + 2], st1[:, 3 * i + 2:3 * i + 3], D, [st1_b[i]])

    def p1_rest(i):
        sl = i % NXS
        hs_ = i % 2
        op("dve", lambda e: e.tensor_scalar_mul(hb[hs_][:], xt[sl][:], st1[:, 3 * i + 2:3 * i + 3]),
           reads=[xt_b[sl], st1_b[i]], writes=[hb_b[hs_]])
        pb = i % 2
        for kc in range(8):
            op("pe", lambda e: e.transpose(out=psbf(pb)[:, kc * 128:(kc + 1) * 128], in_=hb[hs_][:, kc * 128:(kc + 1) * 128],
                                           identity=ident),
               reads=[hb_b[hs_], cst_b], writes=[psb[pb]])

    def p1_evac(i):
        pb = i % 2
        for kc in range(8):
            op("dve", lambda e: e.tensor_scalar_mul(hT[:, kc, i * 128:(i + 1) * 128], psbf(pb)[:, kc * 128:(kc + 1) * 128],
                                                    g1t[:, kc:kc + 1]),
               reads=[g1_b], writes=[psb[pb], hT_b[i]])

    def v_tile(g, i):
        pb = 2 + i % 2
        for kc in range(8):
            op("pe", lambda e: e.matmul(ps[pb][:, :], lhsT=hT[:, kc, i * 128:(i + 1) * 128], rhs=wv[:, kc, :],
                                        start=(kc == 0), stop=(kc == 7)),
               reads=[hT_b[i], wv_b], writes=[psb[pb]])
        if g == 0:
            op("act", lambda e: e.activation(out=VA[:, i, :, 0:64], in_=ps[pb][:, :].rearrange("p (h d) -> p h d", h=8),
                                             func=AF.Copy),
               writes=[psb[pb], V_b[i]])
        else:
            op("dve", lambda e: e.tensor_copy(VB[:, i, :], ps[pb][:, :]), writes=[psb[pb], V_b[i]])

    for i in range(min(2, NT)):
        p1_load(i)
    p1_stats(0)
    for i in range(NT):
        if i + 2 < NT:
            p1_load(i + 2)
        if i + 1 < NT:
            p1_stats(i + 1)
        p1_rest(i)
        if i >= 1:
            p1_evac(i - 1)
        if i >= 2:
            v_tile(0, i - 2)
    p1_evac(NT - 1)
    for i in range(max(NT - 2, 0), NT):
        v_tile(0, i)
    sc.barrier()
    ar.release(wv_mark)
    wqk = [ar.alloc("wqk", [128, 2, 8, 128], BF16) for _ in range(2)]
    wqk_b = [Buf() for _ in range(2)]
    s_wqk = [sc.dma_sem(f"wqk{i}") for i in range(2)]
    p12_mark = ar.mark()

    def v_proj(g):
        for i in range(NT):
            v_tile(g, i)


    def load_wqk(pr):
        dma("pool", s_wqk[pr % 2], wqk[pr % 2][:], wqk_d[pr], writes=[wqk_b[pr % 2]])

    def qk_proj_group(pr, tg, which=(0, 1), part=(0, 1), kcs=range(8)):
        w = wqk[pr % 2]
        for qk in which:
            pb = qk
            if 0 in part:
                for kc in kcs:
                    op("pe", lambda e: e.matmul(ps[pb][:, :], lhsT=w[:, qk, kc, :], rhs=hT[:, kc, tg * 512:(tg + 1) * 512],
                                                start=(kc == 0), stop=(kc == 7)),
                       reads=[wqk_b[pr % 2]] + hT_b[tg * 4:(tg + 1) * 4], writes=[psb[pb]])
            if 1 not in part:
                continue
            if qk == 0:
                op("dve", lambda e: e.tensor_scalar_mul(qT[:, tg * 512:(tg + 1) * 512], ps[pb][:, :], 0.125),
                   writes=[psb[pb], q_b[tg]])
            else:
                op("dve", lambda e: e.tensor_copy(kT[:, tg * 512:(tg + 1) * 512], ps[pb][:, :]),
                   writes=[psb[pb], k_b[tg]])

    load_wqk(0)
    braw = ar.alloc("braw", [128, 2, 128], F32)
    braw_b = Buf()
    s_braw = sc.dma_sem("braw")
    biasX = [ar.alloc("biasX", [128, 3, 128], F32) for _ in range(2)]
    biasX_b = [Buf() for _ in range(2)]
    t3 = [ar.alloc("t3", [128, 3, 128], F32) for _ in range(2)]
    t3_b = [Buf() for _ in range(2)]
    pT = [ar.alloc("pT", [128, 5, 128], BF16) for _ in range(3)]
    pT_b = [Buf() for _ in range(3)]
    yb = [ar.alloc("yb", [128, 128], BF16) for _ in range(2)]
    yb_b = [Buf() for _ in range(2)]
    rec = ar.alloc("rec", [128, 4], F32)
    rec_b = [Buf() for _ in range(4)]
    ysq = ar.alloc("ysq", [128, 64], F32)
    ysq_b = Buf()
    POS = {0: 0, 3: 1, 4: 2, 1: 3, 2: 4}

    def a_stage1(pr, n, c, hd, part):
        h = pr * 2 + hd
        hs = slice(hd * 64, (hd + 1) * 64)
        sl = n % 2
        ba, bb = (2, 3) if sl == 0 else (4, 5)
        psl = n % 3
        js = [j for j in range(5) if c - 4 + j >= 0]
        lo = 0 if 0 in js else 1
        if part == 0:
            for j in js:
                kt = c - 4 + j
                if j in (1, 2):
                    bank, slot = ba, j - 1
                else:
                    bank, slot = bb, POS[j]
                op("pe", lambda e: e.matmul(ps[bank][:, slot * 128:(slot + 1) * 128], lhsT=kT[hs, kt * 128:(kt + 1) * 128],
                                            rhs=qT[hs, c * 128:(c + 1) * 128], start=True, stop=True),
                   reads=[k_b[kt // 4], q_b[c // 4]], writes=[psb[bank]])
            op("dve", lambda e: e.tensor_tensor(t3[sl][:, lo:3, :], ps[bb][:, lo * 128:384].rearrange("p (j q) -> p j q", q=128),
                                                biasX[hd][:, lo:3, :], ALU.add),
               reads=[biasX_b[hd]], writes=[psb[bb], t3_b[sl]])
            return
        op("act", lambda e: e.activation(out=pT[psl][:, lo:3, :], in_=t3[sl][:, lo:3, :], func=AF.Exp),
           reads=[t3_b[sl]], writes=[pT_b[psl]])
        ja = [j for j in js if j in (1, 2)]
        if ja:
            a0_ = ja[0] - 1
            op("act", lambda e: e.activation(out=pT[psl][:, 3 + a0_:5, :],
                                             in_=ps[ba][:, a0_ * 128:256].rearrange("p (j q) -> p j q", q=128),
                                             func=AF.Exp),
               writes=[psb[ba], pT_b[psl]])

    def a_stage2(pr, n, c, hd):
        h = pr * 2 + hd
        hs = slice(hd * 64, (hd + 1) * 64)
        psl = n % 3
        ob = 6 + n % 2
        ybs = c % 2
        js = [j for j in range(5) if c - 4 + j >= 0]
        for n_, j in enumerate(js):
            kt = c - 4 + j
            op("pe", lambda e: e.matmul(ps[ob][:, 0:65], lhsT=pT[psl][:, POS[j], :], rhs=VA[:, kt, h, :],
                                        start=(n_ == 0), stop=(n_ == len(js) - 1)),
               reads=[pT_b[psl], V_b[kt]], writes=[psb[ob]])
        ri = n % 4
        op("dve", lambda e: e.reciprocal(rec[:, ri:ri + 1], ps[ob][:, 64:65]), writes=[psb[ob], rec_b[ri]])
        op("act", lambda e: e.activation(out=ysq[:], in_=ps[ob][:, 0:64], func=AF.Square, scale=rec[:, ri:ri + 1],
                                         accum_out=ssqA[:, c, h:h + 1]),
           reads=[rec_b[ri]], writes=[psb[ob], ysq_b, ssqA_b])
        op("dve", lambda e: e.tensor_scalar_mul(yb[ybs][:, hs], ps[ob][:, 0:64], rec[:, ri:ri + 1]),
           reads=[rec_b[ri]], writes=[psb[ob], yb_b[ybs]])

    def a_stage3(pr, c):
        ybs = c % 2
        tb = c % 2
        op("pe", lambda e: e.transpose(out=psbf(tb)[:, 0:128], in_=yb[ybs][:], identity=ident),
           reads=[yb_b[ybs], cst_b], writes=[psb[tb]])
        op("dve", lambda e: e.tensor_scalar_mul(yT[:, pr, c * 128:(c + 1) * 128], psbf(tb)[:, 0:128], gab[:, pr:pr + 1]),
           reads=[cst_b], writes=[psb[tb], yT_b[pr][c]])

    gn = 0
    for pr in range(4):
        if pr + 1 < 8:
            load_wqk(pr + 1)
        if pr == 1:
            dma("pool", s_wv, wv[:], wv_d[1], writes=[wv_b])
        qk_proj_group(pr, 0)
        for hd in range(2):
            h = pr * 2 + hd
            bx = biasX[hd]
            dma("sp", s_braw, braw[:], biasT_d[h], writes=[braw_b])
            op("dve", lambda e: e.tensor_copy(bx[:, 0, :], maskA[:, 0, :]), reads=[cst_b], writes=[biasX_b[hd]])
            op("dve", lambda e: e.tensor_scalar_sub(bx[:, 1, :], braw[:, 0, :], c256[:, h:h + 1]),
               reads=[braw_b, cst_b], writes=[biasX_b[hd]])
            op("dve", lambda e: e.scalar_tensor_tensor(out=bx[:, 2, :], in0=braw[:, 1, :], scalar=c256[:, h:h + 1],
                                                       in1=maskA[:, 1, :], op0=ALU.subtract, op1=ALU.add),
               reads=[braw_b, cst_b], writes=[biasX_b[hd]])
        iters = [(c, hd) for c in range(NT) for hd in range(2)]
        a_stage1(pr, gn, *iters[0], 0)
        if len(iters) > 1:
            a_stage1(pr, gn + 1, *iters[1], 0)
        a_stage1(pr, gn, *iters[0], 1)
        for i_, (c, hd) in enumerate(iters):
            if i_ + 2 < len(iters):
                a_stage1(pr, gn + 2, *iters[i_ + 2], 0)
            if i_ + 1 < len(iters):
                a_stage1(pr, gn + 1, *iters[i_ + 1], 1)
            a_stage2(pr, gn, c, hd)
            if hd == 1 and c >= 1:
                a_stage3(pr, c - 1)
            if c // 4 + 1 < NG:
                m_ = (c % 4) * 2 + hd
                parts = [range(0, 3), range(3, 6), range(6, 8)]
                if m_ < 3:
                    qk_proj_group(pr, c // 4 + 1, which=(0,), part=(0,), kcs=parts[m_])
                    if m_ == 2:
                        qk_proj_group(pr, c // 4 + 1, which=(0,), part=(1,))
                if 2 <= m_ < 5:
                    qk_proj_group(pr, c // 4 + 1, which=(1,), part=(0,), kcs=parts[m_ - 2])
                    if m_ == 4:
                        qk_proj_group(pr, c // 4 + 1, which=(1,), part=(1,))
            gn += 1
        a_stage3(pr, NT - 1)
    sc.barrier()
    ar.release(p12_mark)

    v_proj(1)
    sc.barrier()
    E2 = ar.alloc_at("E2", [128, 2, 512], F32, wv_off)
    E2_b = Buf()
    SP2 = ar.alloc("SP2", [128, 2, 512], BF16)
    SP2_b = Buf()
    SPs2 = ar.alloc_at("SPs2", [128, 2, 512], F32, wv_off + 4096)
    SPs2_b = Buf()
    SPsb2 = [ar.alloc("SPsb2", [128, 2, 512], BF16) for _ in range(2)]
    SPsb2_b = [Buf() for _ in range(2)]
    AT2 = [ar.alloc("AT2", [128, 2, 512], BF16) for _ in range(2)]
    AT2_b = [Buf() for _ in range(2)]
    sq = SPs2[:, 0, :]
    sq_b = SPs2_b
    XB0 = [2, 4]
    hsl = [slice(0, 64), slice(64, 128)]
    gstep = 0
    pend_ssq = []

    def ssq_mm(ti):
        op("pe", lambda e: e.matmul(ps[7][:, ti:ti + 1], lhsT=sq[:, ti * 128:(ti + 1) * 128], rhs=onesf[:],
                                    start=True, stop=True),
           reads=[sq_b, small_c], writes=[psb[7]])

    for pr in range(4, 8):
        if pr + 1 < 8:
            load_wqk(pr + 1)
        qk_proj_group(pr, 0)
        steps = []
        for G in range(NG):
            kts = list(range(4 * G + 3, -1, -1))
            for n, kt in enumerate(kts):
                c0 = 128 * max(kt - 4 * G, 0)
                c0p = 128 * max(kts[n - 1] - 4 * G, 0) if n > 0 else None
                steps.append((G, n, kt, c0, c0p, n == len(kts) - 1))

        def emit_qk(i):
            G, n, kt, c0, _, _ = steps[i]
            for hd in range(2):
                xb_ = XB0[(gstep + i) % 2] + hd
                op("pe", lambda e: e.matmul(ps[xb_][:, c0:512], lhsT=kT[hsl[hd], kt * 128:(kt + 1) * 128],
                                            rhs=qT[hsl[hd], G * 512 + c0:(G + 1) * 512], start=True, stop=False,
                                            skip_group_check=True),
                   reads=[k_b[kt // 4], q_b[G]], writes=[psb[xb_]])
                if kt >= 4 * G:
                    op("pe", lambda e: e.matmul(ps[xb_][:, c0:c0 + 128], lhsT=ident, rhs=maskneg, start=False, stop=False,
                                                skip_group_check=True),
                       reads=[cst_b], writes=[psb[xb_]])

        def emit_E(i):
            c0 = steps[i][3]
            x0 = XB0[(gstep + i) % 2]
            op("act", lambda e: e.activation(out=E2[:, :, c0:512], in_=ps2(x0)[:, :, c0:512], func=AF.Exp),
               writes=[psb[x0], psb[x0 + 1], E2_b])

        emit_qk(0)
        emit_E(0)
        emit_qk(1)
        for i, (G, n, kt, c0, c0_prev, last) in enumerate(steps):
            par = (gstep + i) % 2
            x0 = XB0[par]
            if G + 1 < NG and G == 0 and n in (0, 1):
                pj = (n, range(8), True)
            elif G + 1 < NG and G == 1 and n in (0, 1, 2, 3):
                pj = (n // 2, range(4) if n % 2 == 0 else range(4, 8), n % 2 == 1)
            elif G + 1 < NG and G >= 2 and n < 8:
                pj = (n // 4, range(2 * (n % 4), 2 * (n % 4) + 2), n % 4 == 3)
            else:
                pj = None
            op("act", lambda e: e.activation(out=SP2[:, :, c0:512], in_=E2[:, :, c0:512], func=AF.Ln, bias=1.0),
               reads=[E2_b], writes=[SP2_b])
            if i + 1 < len(steps):
                emit_E(i + 1)
            for hd in range(2):
                xb_ = x0 + hd
                op("pe", lambda e: e.matmul(ps[xb_][:, c0:512], lhsT=negU, rhs=SP2[:, hd, c0:512], start=False, stop=True,
                                            skip_group_check=True),
                   reads=[SP2_b, cst_b], writes=[psb[xb_]])
            if pr == 7 and G == NG - 1 and n == 0:
                for p8 in range(8):
                    dma("pool", s_wo, wo[:, p8, :], wo_d[:, p8, :], writes=[wo_b] + hT_b)
            if pend_ssq and pend_ssq[0] != (pr, G) and n < 4:
                ssq_mm(3 - n)
            if pj is not None:
                qk_proj_group(pr, G + 1, which=(pj[0],), part=(0,), kcs=pj[1])
            op("act", lambda e: e.activation(out=AT2[par][:, :, c0:512], in_=ps2(x0)[:, :, c0:512], func=AF.Exp),
               writes=[psb[x0], psb[x0 + 1], AT2_b[par]])
            for hd in range(2):
                hb_ = (pr - 4) * 2 + hd
                op("pe", lambda e: e.matmul(ps[6][hsl[hd], c0:512], lhsT=VB[:, kt, hb_ * 64:(hb_ + 1) * 64],
                                            rhs=AT2[par][:, hd, c0:512], start=(n == 0), stop=last, skip_group_check=True),
                   reads=[AT2_b[par], V_b[kt]], writes=[psb[6]])
            if i + 2 < len(steps):
                emit_qk(i + 2)
            if not last:
                cn = 512 if c0_prev is None else c0_prev
                if cn > c0:
                    op("dve", lambda e: e.tensor_copy(SPs2[:, :, c0:cn], SP2[:, :, c0:cn]), reads=[SP2_b], writes=[SPs2_b])
                if cn < 512:
                    op("dve", lambda e: e.tensor_tensor(SPs2[:, :, cn:512], SPs2[:, :, cn:512], SP2[:, :, cn:512], ALU.add),
                       reads=[SP2_b, SPs2_b], writes=[SPs2_b])
                op("dve", lambda e: e.tensor_copy(SPsb2[par][:, :, c0:512], SPs2[:, :, c0:512]), reads=[SPs2_b],
                   writes=[SPsb2_b[par]])
                xn = XB0[1 - par]
                for hd in range(2):
                    op("pe", lambda e: e.matmul(ps[xn + hd][:, c0:512], lhsT=negones, rhs=SPsb2[par][:, hd, c0:512],
                                                start=False, stop=False, skip_group_check=True),
                       reads=[SPsb2_b[par], cst_b], writes=[psb[xn + hd]])
            if pj is not None and pj[2]:
                qk_proj_group(pr, G + 1, which=(pj[0],), part=(1,))
            if last:
                op("act", lambda e: e.activation(out=sq, in_=ps[6][:, :], func=AF.Square), writes=[psb[6], sq_b])
                op("dve", lambda e: e.tensor_scalar_mul(yT[:, pr, G * 512:(G + 1) * 512], ps[6][:, :], gab[:, pr:pr + 1]),
                   reads=[cst_b], writes=[psb[6]] + yT_b[pr][4 * G:4 * G + 4])
                if G + 1 < NG:
                    pend_ssq.append((pr, G))
                else:
                    for ti in range(4):
                        ssq_mm(ti)
                    op("dve", lambda e: e.tensor_copy(ssqB[:, 4 * G:4 * G + 4, pr - 4], ps[7][:, 0:4]),
                       writes=[psb[7], ssqB_b])
            if pend_ssq and not last and n == 3 and pend_ssq[0] != (pr, G):
                ppr, pG = pend_ssq.pop(0)
                op("dve", lambda e: e.tensor_copy(ssqB[:, 4 * pG:4 * pG + 4, ppr - 4], ps[7][:, 0:4]), writes=[psb[7], ssqB_b])
        gstep += len(steps)

    sc.barrier()
    if dbg is not None:
        s_dbg = sc.dma_sem("dbg")
        dma("sp", s_dbg, dbg_d[:], yT[:].rearrange("p a s -> p (a s)"))
        dma("sp", s_dbg, dbg2_d[:, 0:NT * 8], ssqA[:].rearrange("p a s -> p (a s)"))
        dma("sp", s_dbg, dbg2_d[:, NT * 8:NT * 12], ssqB[:].rearrange("p a s -> p (a s)"))
        sc.finish([s_dbg])
    ar.release(persist_mark)
    ar.off += 8 * D * 2
    wfo = ar.alloc("wfo", [128, NJ, D], BF16)
    actT = ar.alloc("actT", [128, NJ, 512], BF16)
    h2T = ar.alloc("h2T", [128, 8, 512], BF16)
    x1 = ar.alloc("x1", [128, 5, D], F32)
    g2t = ar.alloc("g2t", [128, 8], F32)
    gfb = ar.alloc("gfb", [128, D], F32)
    wfi = [ar.alloc("wfi", [128, 2, 8, 128], BF16) for _ in range(3)]
    cw = ar.alloc("cw", [128, 2 * NJ, 3], F32)
    cb = ar.alloc("cb", [128, 2 * NJ], F32)
    halo = ar.alloc("halo", [128, 2 * NJ, 2], F32)
    htmp = ar.alloc("htmp", [128, 4], F32)
    htmp_b = Buf()
    corr = ar.alloc("corr", [128, 2 * NJ, 2], F32)
    corr_b = [Buf() for _ in range(2 * NJ)]
    cg = ar.alloc("cg", [128, 512], F32)
    cv2 = [ar.alloc("cv", [128, 512], F32) for _ in range(2)]
    sg = ar.alloc("sg", [128, 512], F32)
    junk3 = sg[:].bitcast(BF16)
    h2b2 = [ar.alloc("h2b", [128, D], BF16) for _ in range(2)]
    rA = ar.alloc("rA", [128, NT], F32)
    rB = ar.alloc("rB", [128, NT], F32)
    st3 = ar.alloc("st3", [128, 2 * NT], F32)
    w3_b = Buf(); actT_b = [Buf() for _ in range(NJ)]; h2T_b = [Buf() for _ in range(4)]
    x1_b = [Buf() for _ in range(5)]; wfi_b = [Buf(), Buf(), Buf()]; halo_b = [Buf() for _ in range(2 * NJ)]
    cg_b = Buf(); cv2_b = [Buf(), Buf()]; sg_b = Buf(); h2b2_b = [Buf(), Buf()]; r_b = Buf()
    st3_b = [Buf() for _ in range(NT)]
    s_w3 = sc.dma_sem("w3")
    s_wfo = sc.dma_sem("wfo")
    wfo_b = Buf()
    s_wfi = [sc.dma_sem(f"wfi{i}") for i in range(3)]
    s_x1 = [sc.dma_sem(f"x1_{i}") for i in range(5)]
    s_out = [sc.dma_sem(f"out{i}") for i in range(5)]
    dma("sp", s_w3, g2t[:], g2t_d[:], writes=[w3_b])
    dma("sp", s_w3, gfb[:], gfb_d[:], writes=[w3_b])
    dma("sp", s_w3, cw[:], cw_d[:], writes=[w3_b])
    dma("sp", s_w3, cb[:], cb_d[:], writes=[w3_b])
    op("dve", lambda e: e.tensor_reduce(out=rA[:, 0:NT], in_=ssqA[:], axis=AX.X, op=ALU.add), reads=[ssqA_b], writes=[r_b])
    op("dve", lambda e: e.tensor_reduce(out=rB[:, 0:NT], in_=ssqB[:], axis=AX.X, op=ALU.add), reads=[ssqB_b], writes=[r_b])
    rstd_ops(rA[:, 0:NT], rA[:, 0:NT], rA[:, 0:NT], 512, [r_b])
    rstd_ops(rB[:, 0:NT], rB[:, 0:NT], rB[:, 0:NT], 512, [r_b])

    def a_ops(G, ti):
        t = 4 * G + ti
        xs = t % 5
        hbuf, hbuf_b = h2b2[ti % 2], h2b2_b[ti % 2]
        dma("sp", s_x1[xs], x1[:, xs, :], x_d[t * 128:(t + 1) * 128, :], writes=[x1_b[xs]])
        for grp in range(2):
            for hf in range(2):
                bank = grp * 2 + hf
                for n_ in range(4):
                    pr = grp * 4 + n_
                    op("pe", lambda e: e.matmul(ps[bank][:, :], lhsT=yT[:, pr, t * 128:(t + 1) * 128],
                                                rhs=wo[:, pr, hf * 512:(hf + 1) * 512], start=(n_ == 0), stop=(n_ == 3)),
                       reads=[yT_b[pr][t], wo_b], writes=[psb[bank]])
        for grp in range(2):
            rr = rA if grp == 0 else rB
            for hf in range(2):
                bank = grp * 2 + hf
                op("dve", lambda e: e.scalar_tensor_tensor(out=x1[:, xs, hf * 512:(hf + 1) * 512], in0=ps[bank][:, :],
                                                           scalar=rr[:, t:t + 1],
                                                           in1=x1[:, xs, hf * 512:(hf + 1) * 512], op0=ALU.mult, op1=ALU.add),
                   reads=[r_b], writes=[psb[bank], x1_b[xs]])
        op("act", lambda e: e.activation(out=junk3, in_=x1[:, xs, :], func=AF.Square, accum_out=st3[:, 2 * t:2 * t + 1]),
           reads=[x1_b[xs]], writes=[sg_b, st3_b[t]])
        rstd_ops(st3[:, 2 * t:2 * t + 1], st3[:, 2 * t:2 * t + 1], st3[:, 2 * t:2 * t + 1], D, [st3_b[t]])
        op("dve", lambda e: e.tensor_scalar_mul(hbuf[:], x1[:, xs, :], st3[:, 2 * t:2 * t + 1]),
           reads=[x1_b[xs], st3_b[t]], writes=[hbuf_b])

    def a_tr(G, ti):
        hbuf, hbuf_b = h2b2[ti % 2], h2b2_b[ti % 2]
        pb = 6 + ti % 2
        for kc in range(8):
            op("pe", lambda e: e.transpose(out=psbf(pb)[:, kc * 128:(kc + 1) * 128], in_=hbuf[:, kc * 128:(kc + 1) * 128],
                                           identity=ident),
               reads=[hbuf_b, cst_b], writes=[psb[pb]])
        for kc in range(8):
            op("act", lambda e: e.activation(out=h2T[:, kc, ti * 128:(ti + 1) * 128], in_=psbf(pb)[:, kc * 128:(kc + 1) * 128],
                                             func=AF.Identity, scale=g2t[:, kc:kc + 1]),
               reads=[w3_b], writes=[psb[pb], h2T_b[ti]])

    def wfi_load(gj):
        if gj < NG * NJ:
            sl = gj % 3
            dma("pool", s_wfi[sl], wfi[sl][:], wfi_d[gj % NJ], writes=[wfi_b[sl]])

    def b_ffn_in(G):
        if G == 0:
            wfi_load(0)
            wfi_load(1)
        for j in range(NJ):
            gj = G * NJ + j
            sl = gj % 3
            wfi_load(gj + 2)
            if G == 0:
                dma("pool", s_wfo, wfo[:, j, :], wfo_d[:, j, :], writes=[wfo_b])
            banks = (2, 3) if j % 2 == 0 else (0, 1)
            for gv in range(2):
                for kc in range(8):
                    op("pe", lambda e: e.matmul(ps[banks[gv]][:, :], lhsT=wfi[sl][:, gv, kc, :], rhs=h2T[:, kc, :],
                                                start=(kc == 0), stop=(kc == 7)),
                       reads=[wfi_b[sl]] + h2T_b, writes=[psb[banks[gv]]])
            for gv in range(2):
                idx = gv * NJ + j
                cx, cx_b = (cg, cg_b) if gv == 0 else (cv2[j % 2], cv2_b[j % 2])
                u = ps[banks[gv]]
                ub = psb[banks[gv]]
                op("act", lambda e: e.activation(out=cx[:], in_=u[:, :], func=AF.Identity, scale=cw[:, idx, 2:3],
                                                 bias=cb[:, idx:idx + 1]),
                   reads=[w3_b], writes=[ub, cx_b])
                if G > 0:
                    op("pool", lambda e: e.tensor_tensor(cx[:, 0:2], cx[:, 0:2], corr[:, idx, :], ALU.add),
                       reads=[corr_b[idx]], writes=[cx_b])
                if G + 1 < NG:
                    op("act", lambda e: e.activation(out=halo[:, idx, :], in_=u[:, 510:512], func=AF.Copy),
                       writes=[ub, halo_b[idx]])
                op("dve", lambda e: e.scalar_tensor_tensor(out=cx[:, 1:512], in0=u[:, 0:511], scalar=cw[:, idx, 1:2],
                                                           in1=cx[:, 1:512], op0=ALU.mult, op1=ALU.add),
                   reads=[w3_b], writes=[ub, cx_b])
                op("dve", lambda e: e.scalar_tensor_tensor(out=cx[:, 2:512], in0=u[:, 0:510], scalar=cw[:, idx, 0:1],
                                                           in1=cx[:, 2:512], op0=ALU.mult, op1=ALU.add),
                   reads=[w3_b], writes=[ub, cx_b])
            op("act", lambda e: e.activation(out=sg[:], in_=cg[:], func=AF.Silu), reads=[cg_b], writes=[sg_b])
            op("dve", lambda e: e.tensor_tensor(actT[:, j, :], sg[:], cv2[j % 2][:], ALU.mult), reads=[sg_b, cv2_b[j % 2]],
               writes=[actT_b[j]])
            if G + 1 < NG:
                for gv in range(2):
                    idx = gv * NJ + j
                    op("pool", lambda e: e.tensor_scalar_mul(corr[:, idx, 0:2], halo[:, idx, 0:2], cw[:, idx, 0:1]),
                       reads=[w3_b, halo_b[idx]], writes=[corr_b[idx]])
                    op("pool", lambda e: e.tensor_scalar_mul(htmp[:, 0:1], halo[:, idx, 1:2], cw[:, idx, 1:2]),
                       reads=[w3_b, halo_b[idx]], writes=[htmp_b])
                    op("pool", lambda e: e.tensor_tensor(corr[:, idx, 0:1], corr[:, idx, 0:1], htmp[:, 0:1], ALU.add),
                       reads=[htmp_b], writes=[corr_b[idx]])

    def c_ffn_out(G, ti):
        t = 4 * G + ti
        xs = t % 5
        for hf in range(2):
            bank = 4 + hf
            for j in range(NJ):
                op("pe", lambda e: e.matmul(ps[bank][:, :], lhsT=actT[:, j, ti * 128:(ti + 1) * 128],
                                            rhs=wfo[:, j, hf * 512:(hf + 1) * 512], start=(j == 0), stop=(j == NJ - 1)),
                   reads=[actT_b[j], wfo_b], writes=[psb[bank]])
            op("dve", lambda e: e.tensor_tensor(x1[:, xs, hf * 512:(hf + 1) * 512], ps[bank][:, :],
                                                x1[:, xs, hf * 512:(hf + 1) * 512], ALU.add),
               writes=[psb[bank], x1_b[xs]])
        op("act", lambda e: e.activation(out=junk3, in_=x1[:, xs, :], func=AF.Square, accum_out=st3[:, 2 * t + 1:2 * t + 2]),
           reads=[x1_b[xs]], writes=[sg_b, st3_b[t]])
        rstd_ops(st3[:, 2 * t + 1:2 * t + 2], st3[:, 2 * t + 1:2 * t + 2], st3[:, 2 * t + 1:2 * t + 2], D, [st3_b[t]])
        op("dve", lambda e: e.scalar_tensor_tensor(out=x1[:, xs, :], in0=x1[:, xs, :], scalar=st3[:, 2 * t + 1:2 * t + 2],
                                                   in1=gfb[:], op0=ALU.mult, op1=ALU.mult),
           reads=[st3_b[t], w3_b], writes=[x1_b[xs]])
        dma("sp", s_out[xs], out_d[t * 128:(t + 1) * 128, :], x1[:, xs, :], reads=[x1_b[xs]])

    a_ops(0, 0)
    for ti in range(1, 4):
        a_ops(0, ti)
        a_tr(0, ti - 1)
    a_tr(0, 3)
    for G in range(NG):
        b_ffn_in(G)
        nxt = G + 1 < NG
        if nxt:
            a_ops(G + 1, 0)
        for ti in range(4):
            c_ffn_out(G, ti)
            if nxt:
                a_tr(G + 1, ti)
                if ti + 1 < 4:
                    a_ops(G + 1, ti + 1)
    sc.finish(s_out)
    build_nc.stats = (sc.n_inst, sc.n_wait)
    return nc


def host_inputs(x, norm1_g, w_in, rel_bias, norm_a_g, norm_b_g, w_out, norm2_g, w_ffn_in, conv_w, conv_b,
                w_ffn_out, final_g):
    f = np.float32
    w_in = np.asarray(w_in, f)[0]
    wk = w_in.reshape(8, 128, 3072)
    qbase = [pr * 128 for pr in range(4)] + [1536 + pr * 128 for pr in range(4)]
    kbase = [512 + pr * 128 for pr in range(4)] + [2048 + pr * 128 for pr in range(4)]
    wqk = np.empty((8, 128, 2, 8, 128), f)
    for pr in range(8):
        wqk[pr, :, 0] = wk[:, :, qbase[pr]:qbase[pr] + 128].transpose(1, 0, 2)
        wqk[pr, :, 1] = wk[:, :, kbase[pr]:kbase[pr] + 128].transpose(1, 0, 2)
    wv = np.empty((2, 128, 8, 512), f)
    wv[0] = wk[:, :, 1024:1536].transpose(1, 0, 2)
    wv[1] = wk[:, :, 2560:3072].transpose(1, 0, 2)
    bc = lambda v: np.ascontiguousarray(np.broadcast_to(np.asarray(v, f).reshape(1, -1), (128, np.asarray(v).size)))
    gfb = bc(final_g)
    g1t = np.ascontiguousarray(np.asarray(norm1_g, f)[0].reshape(8, 128).T)
    g2t = np.ascontiguousarray(np.asarray(norm2_g, f)[0].reshape(8, 128).T)
    rb = np.asarray(rel_bias, f)[0]
    k = np.arange(128)[:, None]; q = np.arange(128)[None, :]
    biasT = np.empty((8, 128, 2, 128), f)
    for jj, j in enumerate((3, 4)):
        idx = np.clip((4 - j) * 128 + q - k, -128, 128) + 128
        biasT[:, :, jj, :] = rb[:, idx]
    c256 = bc(rb[:, 256])
    gcat = np.concatenate([np.asarray(norm_a_g, f)[0], np.asarray(norm_b_g, f)[0]])
    gab = np.ascontiguousarray(gcat.reshape(8, 128).T)
    wo = np.ascontiguousarray(np.asarray(w_out, f)[0].reshape(8, 128, D).transpose(1, 0, 2))
    wfi_full = np.asarray(w_ffn_in, f)[0].reshape(8, 128, 2, NJ, 128)
    wfi = np.ascontiguousarray(wfi_full.transpose(3, 1, 2, 0, 4))
    cwf = np.asarray(conv_w, f)[0].reshape(3, 2 * NJ, 128)
    cw = np.ascontiguousarray(cwf.transpose(2, 1, 0))
    cb = np.ascontiguousarray(np.asarray(conv_b, f)[0].reshape(2 * NJ, 128).T)
    wfo = np.ascontiguousarray(np.asarray(w_ffn_out, f)[0].reshape(NJ, 128, D).transpose(1, 0, 2))
    cst = np.zeros((128, 6, 128), f)
    cst[:, 0, :] = np.eye(128, dtype=f)
    cst[:, 1, :] = -(k >= q).astype(f)
    cst[:, 2, :] = -1.0
    cst[:, 3, :] = np.where(k >= q, -30000.0, 0.0)
    cst[:, 4, :] = np.where((k < 64) & (q >= 64), NEG, 0.0)
    cst[:, 5, :] = np.where((k >= 64) & (q < 64), NEG, 0.0)
    shared = dict(wqk=wqk, wv=wv, g1t=g1t, g2t=g2t, gfb=gfb, biasT=biasT, c256=c256, gab=gab, wo=wo, wfi=wfi,
                  cw=cw, cb=cb, wfo=wfo, cst=cst)
    return shared


_NC_CACHE = {}


def kernel(x, norm1_g, w_in, rel_bias, norm_a_g, norm_b_g, w_out, norm2_g, w_ffn_in, conv_w, conv_b,
           w_ffn_out, final_g):
    x = np.asarray(x, np.float32)
    B, S, _ = x.shape
    shared = host_inputs(x, norm1_g, w_in, rel_bias, norm_a_g, norm_b_g, w_out, norm2_g, w_ffn_in, conv_w,
                         conv_b, w_ffn_out, final_g)
    if S not in _NC_CACHE:
        _NC_CACHE[S] = build_nc(S)
    nc = _NC_CACHE[S]
    in_maps = [dict(shared, x=np.ascontiguousarray(x[b])) for b in range(B)]
    res = run_bass_kernel_spmd(nc, in_maps, core_ids=list(range(B)))
    return np.stack([np.asarray(r["out"], np.float32) for r in res.results], axis=0)
```

```python
import numpy as np
import concourse.bass as bass
import concourse.mybir as mybir
from concourse.bass_utils import run_bass_kernel_spmd

F32 = mybir.dt.float32
BF16 = mybir.dt.bfloat16
AF = mybir.ActivationFunctionType
ALU = mybir.AluOpType
AX = mybir.AxisListType

D = 1024
DFF = 2816
NJ = DFF // 128
EPS = 1e-6
NEG = -1e30


class Buf:
    __slots__ = ("w", "r")

    def __init__(self):
        self.w = None
        self.r = []


class Sched:
    def __init__(self, nc):
        self.nc = nc
        self.eng = {"pe": nc.tensor, "act": nc.scalar, "dve": nc.vector, "pool": nc.gpsimd, "sp": nc.sync}
        self.semobj = {k: nc.alloc_semaphore("sem_" + k) for k in self.eng}
        self.cnt = {k: 0 for k in self.eng}
        self.waited = {k: {} for k in self.eng}
        self.n_wait = 0
        self.n_inst = 0

    def dma_sem(self, name):
        self.semobj[name] = self.nc.alloc_semaphore("dsem_" + name)
        self.cnt[name] = 0
        return name

    def _needs(self, e, reads, writes):
        deps = {}

        def add(m, same_ok):
            if m is None:
                return
            k, v = m
            if k == e and not same_ok:
                return
            if deps.get(k, 0) < v:
                deps[k] = v

        raw_ok = e != "pe"
        for b in reads:
            add(b.w, raw_ok)
        for b in writes:
            add(b.w, raw_ok)
            for m in b.r:
                add(m, False)
        need = []
        for k, v in deps.items():
            if self.waited[e].get(k, 0) >= v:
                continue
            need.append((k, v))
        return need

    def _mark(self, marker, reads, writes):
        for b in reads:
            b.r.append(marker)
        for b in writes:
            b.w = marker
            b.r = []

    def op(self, e, fn, reads=(), writes=()):
        need = self._needs(e, reads, writes)
        eng = self.eng[e]
        for k, v in need[:-1]:
            eng.wait_ge(self.semobj[k], v)
            self.waited[e][k] = v
            self.n_wait += 1
        inst = fn(eng)
        if need:
            k, v = need[-1]
            inst._wait_ge(self.semobj[k], v)
            self.waited[e][k] = v
        inst.then_inc(self.semobj[e], 1)
        self.cnt[e] += 1
        self.n_inst += 1
        self._mark((e, self.cnt[e]), reads, writes)

    def dma(self, q, sem, out, in_, reads=(), writes=()):
        need = [(k, v) for (k, v) in self._needs(q, reads, writes) if k != sem]
        eng = self.eng[q]
        for k, v in need:
            eng.wait_ge(self.semobj[k], v)
            self.waited[q][k] = v
            self.n_wait += 1
        eng.dma_start(out=out, in_=in_).then_inc(self.semobj[sem], 16)
        self.cnt[sem] += 16
        self.n_inst += 1
        self._mark((sem, self.cnt[sem]), reads, writes)

    def barrier(self):
        for e, eng in self.eng.items():
            for k, c in self.cnt.items():
                if k == e or c == 0:
                    continue
                if self.waited[e].get(k, 0) >= c:
                    continue
                eng.wait_ge(self.semobj[k], c)
                self.waited[e][k] = c
                self.n_wait += 1

    def finish(self, sems):
        eng = self.eng["sp"]
        for k in sems:
            if self.cnt[k] > self.waited["sp"].get(k, 0):
                eng.wait_ge(self.semobj[k], self.cnt[k])


class Arena:
    def __init__(self, nc):
        self.nc = nc
        self.base = (nc.sbuf_base + 63) // 64 * 64
        self.top = nc.sbuf_top
        self.off = self.base
        self.n = 0

    def alloc(self, name, shape, dt):
        nb = int(np.prod(shape[1:])) * (4 if dt == F32 else 2)
        nb = (nb + 63) // 64 * 64
        assert self.off + nb <= self.top, f"SBUF overflow at {name}: {self.off + nb - self.base} > {self.top - self.base}"
        t = self.nc.alloc_sbuf_tensor_at(f"{name}_{self.n}", list(shape), dt, offset=self.off)
        self.n += 1
        self.off += nb
        return t

    def alloc_at(self, name, shape, dt, off):
        t = self.nc.alloc_sbuf_tensor_at(f"{name}_{self.n}", list(shape), dt, offset=off)
        self.n += 1
        return t

    def mark(self):
        return self.off

    def release(self, m):
        self.off = m


def build_nc(S, dbg=None):
    NT = S // 128
    NG = S // 512
    nc = bass.Bass("TRN2", target_bir_lowering=False)

    def din(name, shape):
        return nc.dram_tensor(name, list(shape), F32, kind="ExternalInput").ap()

    x_d = din("x", [S, D])
    wqk_d = din("wqk", [8, 128, 2, 8, 128])
    wv_d = din("wv", [2, 128, 8, 512])
    g1t_d = din("g1t", [128, 8])
    g2t_d = din("g2t", [128, 8])
    gfb_d = din("gfb", [128, D])
    biasT_d = din("biasT", [8, 128, 2, 128])
    c256_d = din("c256", [128, 8])
    gab_d = din("gab", [128, 8])
    wo_d = din("wo", [128, 8, D])
    wfi_d = din("wfi", [NJ, 128, 2, 8, 128])
    cw_d = din("cw", [128, 2 * NJ, 3])
    cb_d = din("cb", [128, 2 * NJ])
    wfo_d = din("wfo", [128, NJ, D])
    cst_d = din("cst", [128, 6, 128])
    out_d = nc.dram_tensor("out", [S, D], F32, kind="ExternalOutput").ap()
    dbg_d = None
    if dbg is not None:
        dbg_d = nc.dram_tensor("dbg", [128, 8 * S], BF16, kind="ExternalOutput").ap()
        dbg2_d = nc.dram_tensor("dbg2", [128, NT * 12], F32, kind="ExternalOutput").ap()

    sc = Sched(nc)
    ar = Arena(nc)
    op, dma = sc.op, sc.dma

    psall = nc.alloc_psum_tensor("psall", [128, 8 * 512], F32)
    ps = [psall[:, i * 512:(i + 1) * 512] for i in range(8)]
    psb = [Buf() for _ in range(8)]

    def psbf(i):
        return ps[i].bitcast(BF16)

    def ps2(b0):
        return psall[:, b0 * 512:(b0 + 2) * 512].rearrange("p (h c) -> p h c", h=2)

    yT = ar.alloc("yT", [128, 8, S], BF16)
    yT_b = [[Buf() for _ in range(NT)] for _ in range(8)]
    cstb = ar.alloc("cstb", [128, 4, 128], BF16)
    maskA = ar.alloc("maskA", [128, 2, 128], F32)
    c256 = ar.alloc("c256", [128, 8], F32)
    gab = ar.alloc("gab", [128, 8], F32)
    ssqA = ar.alloc("ssqA", [128, NT, 8], F32)
    ssqB = ar.alloc("ssqB", [128, NT, 4], F32)
    onesf = ar.alloc("onesf", [128, 1], F32)
    cst_b = Buf(); small_c = Buf(); ssqA_b = Buf(); ssqB_b = Buf()
    ident = cstb[:, 0, :]
    negU = cstb[:, 1, :]
    negones = cstb[:, 2, :]
    maskneg = cstb[:, 3, :]
    s_cst = sc.dma_sem("cst")
    dma("pool", s_cst, cstb[:], cst_d[:, 0:4, :], writes=[cst_b])
    dma("sp", s_cst, maskA[:], cst_d[:, 4:6, :], writes=[cst_b])
    dma("sp", s_cst, c256[:], c256_d[:], writes=[cst_b])
    dma("sp", s_cst, gab[:], gab_d[:], writes=[cst_b])
    op("pool", lambda e: e.memset(onesf[:], 1.0), writes=[small_c])
    persist_mark = ar.mark()

    hT = ar.alloc("hT", [128, 8, S], BF16)
    hT_b = [Buf() for _ in range(NT)]
    Vbytes = NT * 8 * 65 * 2
    v_off = ar.mark()
    VA = ar.alloc("VA", [128, NT, 8, 65], BF16)
    ar.release(v_off)
    VB = ar.alloc("VB", [128, NT, 512], BF16)
    ar.release(v_off + (Vbytes + 63) // 64 * 64)
    V_b = [Buf() for _ in range(NT)]
    qT = ar.alloc("qT", [128, S], BF16)
    kT = ar.alloc("kT", [128, S], BF16)
    q_b = [Buf() for _ in range(NG)]
    k_b = [Buf() for _ in range(NG)]
    p12_mark = ar.mark()

    def rstd_ops(ss_ap, tmp_ap, out_ap, n, bufs):
        op("act", lambda e: e.activation(out=tmp_ap, in_=ss_ap, func=AF.Ln, scale=1.0 / n, bias=EPS),
           reads=bufs, writes=bufs)
        op("act", lambda e: e.activation(out=out_ap, in_=tmp_ap, func=AF.Exp, scale=-0.5),
           reads=bufs, writes=bufs)

    wv_off = ar.mark()
    wv = ar.alloc("wv", [128, 8, 512], BF16)
    wv_b = Buf()
    s_wv = sc.dma_sem("wv")
    dma("pool", s_wv, wv[:], wv_d[0], writes=[wv_b])
    op("pool", lambda e: e.memset(VA[:, :, :, 64:65], 1.0), writes=V_b)
    wv_mark = ar.mark()
    NXS = 3
    xt = [ar.alloc("xt", [128, D], F32) for _ in range(NXS)]
    xt_b = [Buf() for _ in range(NXS)]
    s_x = [sc.dma_sem(f"x{i}") for i in range(NXS)]
    g1t = ar.alloc("g1t", [128, 8], F32)
    g1_b = Buf()
    s_g = sc.dma_sem("g")
    dma("sp", s_g, g1t[:], g1t_d[:], writes=[g1_b])
    hb = [ar.alloc("hb", [128, D], BF16) for _ in range(2)]
    hb_b = [Buf() for _ in range(2)]
    junk = ar.alloc("junk", [128, D], BF16)
    junk_b = Buf()
    st1 = ar.alloc("st1", [128, 3 * NT], F32)
    st1_b = [Buf() for _ in range(NT)]

    def p1_load(i):
        sl = i % NXS
        dma("sp", s_x[sl], xt[sl][:], x_d[i * 128:(i + 1) * 128, :], writes=[xt_b[sl]])

    def p1_stats(i):
        sl = i % NXS
        op("act", lambda e: e.activation(out=junk[:], in_=xt[sl][:], func=AF.Square, accum_out=st1[:, 3 * i:3 * i + 1]),
           reads=[xt_b[sl]], writes=[junk_b, st1_b[i]])
        rstd_ops(st1[:, 3 * i:3 * i + 1], st1[:, 3 * i + 1:3 * i + 2], st1[:, 3 * i + 2:3 * i + 3], D, [st1_b[i]])

    def p1_rest(i):
        sl = i % NXS
        hs_ = i % 2
        op("dve", lambda e: e.tensor_scalar_mul(hb[hs_][:], xt[sl][:], st1[:, 3 * i + 2:3 * i + 3]),
           reads=[xt_b[sl], st1_b[i]], writes=[hb_b[hs_]])
        pb = i % 2
        for kc in range(8):
            op("pe", lambda e: e.transpose(out=psbf(pb)[:, kc * 128:(kc + 1) * 128], in_=hb[hs_][:, kc * 128:(kc + 1) * 128],
                                           identity=ident),
               reads=[hb_b[hs_], cst_b], writes=[psb[pb]])

    def p1_evac(i):
        pb = i % 2
        for kc in range(8):
            op("dve", lambda e: e.tensor_scalar_mul(hT[:, kc, i * 128:(i + 1) * 128], psbf(pb)[:, kc * 128:(kc + 1) * 128],
                                                    g1t[:, kc:kc + 1]),
               reads=[g1_b], writes=[psb[pb], hT_b[i]])

    def v_tile(g, i):
        pb = 2 + i % 2
        for kc in range(8):
            op("pe", lambda e: e.matmul(ps[pb][:, :], lhsT=hT[:, kc, i * 128:(i + 1) * 128], rhs=wv[:, kc, :],
                                        start=(kc == 0), stop=(kc == 7)),
               reads=[hT_b[i], wv_b], writes=[psb[pb]])
        if g == 0:
            op("act", lambda e: e.activation(out=VA[:, i, :, 0:64], in_=ps[pb][:, :].rearrange("p (h d) -> p h d", h=8),
                                             func=AF.Copy),
               writes=[psb[pb], V_b[i]])
        else:
            op("dve", lambda e: e.tensor_copy(VB[:, i, :], ps[pb][:, :]), writes=[psb[pb], V_b[i]])

    for i in range(min(2, NT)):
        p1_load(i)
    p1_stats(0)
    for i in range(NT):
        if i + 2 < NT:
            p1_load(i + 2)
        if i + 1 < NT:
            p1_stats(i + 1)
        p1_rest(i)
        if i >= 1:
            p1_evac(i - 1)
        if i >= 2:
            v_tile(0, i - 2)
    p1_evac(NT - 1)
    for i in range(max(NT - 2, 0), NT):
        v_tile(0, i)
    sc.barrier()
    ar.release(wv_mark)
    wqk = [ar.alloc("wqk", [128, 2, 8, 128], BF16) for _ in range(2)]
    wqk_b = [Buf() for _ in range(2)]
    s_wqk = [sc.dma_sem(f"wqk{i}") for i in range(2)]
    p12_mark = ar.mark()

    def v_proj(g):
        for i in range(NT):
            v_tile(g, i)


    def load_wqk(pr):
        dma("pool", s_wqk[pr % 2], wqk[pr % 2][:], wqk_d[pr], writes=[wqk_b[pr % 2]])

    def qk_proj_group(pr, tg, which=(0, 1), part=(0, 1), kcs=range(8)):
        w = wqk[pr % 2]
        for qk in which:
            pb = qk
            if 0 in part:
                for kc in kcs:
                    op("pe", lambda e: e.matmul(ps[pb][:, :], lhsT=w[:, qk, kc, :], rhs=hT[:, kc, tg * 512:(tg + 1) * 512],
                                                start=(kc == 0), stop=(kc == 7)),
                       reads=[wqk_b[pr % 2]] + hT_b[tg * 4:(tg + 1) * 4], writes=[psb[pb]])
            if 1 not in part:
                continue
            if qk == 0:
                op("dve", lambda e: e.tensor_scalar_mul(qT[:, tg * 512:(tg + 1) * 512], ps[pb][:, :], 0.125),
                   writes=[psb[pb], q_b[tg]])
            else:
                op("dve", lambda e: e.tensor_copy(kT[:, tg * 512:(tg + 1) * 512], ps[pb][:, :]),
                   writes=[psb[pb], k_b[tg]])

    load_wqk(0)
    braw = ar.alloc("braw", [128, 2, 128], F32)
    braw_b = Buf()
    s_braw = sc.dma_sem("braw")
    biasX = [ar.alloc("biasX", [128, 3, 128], F32) for _ in range(2)]
    biasX_b = [Buf() for _ in range(2)]
    t3 = [ar.alloc("t3", [128, 3, 128], F32) for _ in range(2)]
    t3_b = [Buf() for _ in range(2)]
    pT = [ar.alloc("pT", [128, 5, 128], BF16) for _ in range(3)]
    pT_b = [Buf() for _ in range(3)]
    yb = [ar.alloc("yb", [128, 128], BF16) for _ in range(2)]
    yb_b = [Buf() for _ in range(2)]
    rec = ar.alloc("rec", [128, 4], F32)
    rec_b = [Buf() for _ in range(4)]
    ysq = ar.alloc("ysq", [128, 64], F32)
    ysq_b = Buf()
    POS = {0: 0, 3: 1, 4: 2, 1: 3, 2: 4}

    def a_stage1(pr, n, c, hd, part):
        h = pr * 2 + hd
        hs = slice(hd * 64, (hd + 1) * 64)
        sl = n % 2
        ba, bb = (2, 3) if sl == 0 else (4, 5)
        psl = n % 3
        js = [j for j in range(5) if c - 4 + j >= 0]
        lo = 0 if 0 in js else 1
        if part == 0:
            for j in js:
                kt = c - 4 + j
                if j in (1, 2):
                    bank, slot = ba, j - 1
                else:
                    bank, slot = bb, POS[j]
                op("pe", lambda e: e.matmul(ps[bank][:, slot * 128:(slot + 1) * 128], lhsT=kT[hs, kt * 128:(kt + 1) * 128],
                                            rhs=qT[hs, c * 128:(c + 1) * 128], start=True, stop=True),
                   reads=[k_b[kt // 4], q_b[c // 4]], writes=[psb[bank]])
            op("dve", lambda e: e.tensor_tensor(t3[sl][:, lo:3, :], ps[bb][:, lo * 128:384].rearrange("p (j q) -> p j q", q=128),
                                                biasX[hd][:, lo:3, :], ALU.add),
               reads=[biasX_b[hd]], writes=[psb[bb], t3_b[sl]])
            return
        op("act", lambda e: e.activation(out=pT[psl][:, lo:3, :], in_=t3[sl][:, lo:3, :], func=AF.Exp),
           reads=[t3_b[sl]], writes=[pT_b[psl]])
        ja = [j for j in js if j in (1, 2)]
        if ja:
            a0_ = ja[0] - 1
            op("act", lambda e: e.activation(out=pT[psl][:, 3 + a0_:5, :],
                                             in_=ps[ba][:, a0_ * 128:256].rearrange("p (j q) -> p j q", q=128),
                                             func=AF.Exp),
               writes=[psb[ba], pT_b[psl]])

    def a_stage2(pr, n, c, hd):
        h = pr * 2 + hd
        hs = slice(hd * 64, (hd + 1) * 64)
        psl = n % 3
        ob = 6 + n % 2
        ybs = c % 2
        js = [j for j in range(5) if c - 4 + j >= 0]
        for n_, j in enumerate(js):
            kt = c - 4 + j
            op("pe", lambda e: e.matmul(ps[ob][:, 0:65], lhsT=pT[psl][:, POS[j], :], rhs=VA[:, kt, h, :],
                                        start=(n_ == 0), stop=(n_ == len(js) - 1)),
               reads=[pT_b[psl], V_b[kt]], writes=[psb[ob]])
        ri = n % 4
        op("dve", lambda e: e.reciprocal(rec[:, ri:ri + 1], ps[ob][:, 64:65]), writes=[psb[ob], rec_b[ri]])
        op("act", lambda e: e.activation(out=ysq[:], in_=ps[ob][:, 0:64], func=AF.Square, scale=rec[:, ri:ri + 1],
                                         accum_out=ssqA[:, c, h:h + 1]),
           reads=[rec_b[ri]], writes=[psb[ob], ysq_b, ssqA_b])
        op("dve", lambda e: e.tensor_scalar_mul(yb[ybs][:, hs], ps[ob][:, 0:64], rec[:, ri:ri + 1]),
           reads=[rec_b[ri]], writes=[psb[ob], yb_b[ybs]])

    def a_stage3(pr, c):
        ybs = c % 2
        tb = c % 2
        op("pe", lambda e: e.transpose(out=psbf(tb)[:, 0:128], in_=yb[ybs][:], identity=ident),
           reads=[yb_b[ybs], cst_b], writes=[psb[tb]])
        op("dve", lambda e: e.tensor_scalar_mul(yT[:, pr, c * 128:(c + 1) * 128], psbf(tb)[:, 0:128], gab[:, pr:pr + 1]),
           reads=[cst_b], writes=[psb[tb], yT_b[pr][c]])

    gn = 0
    for pr in range(4):
        if pr + 1 < 8:
            load_wqk(pr + 1)
        if pr == 1:
            dma("pool", s_wv, wv[:], wv_d[1], writes=[wv_b])
        qk_proj_group(pr, 0)
        for hd in range(2):
            h = pr * 2 + hd
            bx = biasX[hd]
            dma("sp", s_braw, braw[:], biasT_d[h], writes=[braw_b])
            op("dve", lambda e: e.tensor_copy(bx[:, 0, :], maskA[:, 0, :]), reads=[cst_b], writes=[biasX_b[hd]])
            op("dve", lambda e: e.tensor_scalar_sub(bx[:, 1, :], braw[:, 0, :], c256[:, h:h + 1]),
               reads=[braw_b, cst_b], writes=[biasX_b[hd]])
            op("dve", lambda e: e.scalar_tensor_tensor(out=bx[:, 2, :], in0=braw[:, 1, :], scalar=c256[:, h:h + 1],
                                                       in1=maskA[:, 1, :], op0=ALU.subtract, op1=ALU.add),
               reads=[braw_b, cst_b], writes=[biasX_b[hd]])
        iters = [(c, hd) for c in range(NT) for hd in range(2)]
        a_stage1(pr, gn, *iters[0], 0)
        if len(iters) > 1:
            a_stage1(pr, gn + 1, *iters[1], 0)
        a_stage1(pr, gn, *iters[0], 1)
        for i_, (c, hd) in enumerate(iters):
            if i_ + 2 < len(iters):
                a_stage1(pr, gn + 2, *iters[i_ + 2], 0)
            if i_ + 1 < len(iters):
                a_stage1(pr, gn + 1, *iters[i_ + 1], 1)
            a_stage2(pr, gn, c, hd)
            if hd == 1 and c >= 1:
                a_stage3(pr, c - 1)
            if c // 4 + 1 < NG:
                m_ = (c % 4) * 2 + hd
                parts = [range(0, 3), range(3, 6), range(6, 8)]
                if m_ < 3:
                    qk_proj_group(pr, c // 4 + 1, which=(0,), part=(0,), kcs=parts[m_])
                    if m_ == 2:
                        qk_proj_group(pr, c // 4 + 1, which=(0,), part=(1,))
                if 2 <= m_ < 5:
                    qk_proj_group(pr, c // 4 + 1, which=(1,), part=(0,), kcs=parts[m_ - 2])
                    if m_ == 4:
                        qk_proj_group(pr, c // 4 + 1, which=(1,), part=(1,))
            gn += 1
        a_stage3(pr, NT - 1)
    sc.barrier()
    ar.release(p12_mark)

    v_proj(1)
    sc.barrier()
    E2 = ar.alloc_at("E2", [128, 2, 512], F32, wv_off)
    E2_b = Buf()
    SP2 = ar.alloc("SP2", [128, 2, 512], BF16)
    SP2_b = Buf()
    SPs2 = ar.alloc_at("SPs2", [128, 2, 512], F32, wv_off + 4096)
    SPs2_b = Buf()
    SPsb2 = [ar.alloc("SPsb2", [128, 2, 512], BF16) for _ in range(2)]
    SPsb2_b = [Buf() for _ in range(2)]
    AT2 = [ar.alloc("AT2", [128, 2, 512], BF16) for _ in range(2)]
    AT2_b = [Buf() for _ in range(2)]
    sq = SPs2[:, 0, :]
    sq_b = SPs2_b
    XB0 = [2, 4]
    hsl = [slice(0, 64), slice(64, 128)]
    gstep = 0
    pend_ssq = []

    def ssq_mm(ti):
        op("pe", lambda e: e.matmul(ps[7][:, ti:ti + 1], lhsT=sq[:, ti * 128:(ti + 1) * 128], rhs=onesf[:],
                                    start=True, stop=True),
           reads=[sq_b, small_c], writes=[psb[7]])

    for pr in range(4, 8):
        if pr + 1 < 8:
            load_wqk(pr + 1)
        qk_proj_group(pr, 0)
        steps = []
        for G in range(NG):
            kts = list(range(4 * G + 3, -1, -1))
            for n, kt in enumerate(kts):
                c0 = 128 * max(kt - 4 * G, 0)
                c0p = 128 * max(kts[n - 1] - 4 * G, 0) if n > 0 else None
                steps.append((G, n, kt, c0, c0p, n == len(kts) - 1))

        def emit_qk(i):
            G, n, kt, c0, _, _ = steps[i]
            for hd in range(2):
                xb_ = XB0[(gstep + i) % 2] + hd
                op("pe", lambda e: e.matmul(ps[xb_][:, c0:512], lhsT=kT[hsl[hd], kt * 128:(kt + 1) * 128],
                                            rhs=qT[hsl[hd], G * 512 + c0:(G + 1) * 512], start=True, stop=False,
                                            skip_group_check=True),
                   reads=[k_b[kt // 4], q_b[G]], writes=[psb[xb_]])
                if kt >= 4 * G:
                    op("pe", lambda e: e.matmul(ps[xb_][:, c0:c0 + 128], lhsT=ident, rhs=maskneg, start=False, stop=False,
                                                skip_group_check=True),
                       reads=[cst_b], writes=[psb[xb_]])

        def emit_E(i):
            c0 = steps[i][3]
            x0 = XB0[(gstep + i) % 2]
            op("act", lambda e: e.activation(out=E2[:, :, c0:512], in_=ps2(x0)[:, :, c0:512], func=AF.Exp),
               writes=[psb[x0], psb[x0 + 1], E2_b])

        emit_qk(0)
        emit_E(0)
        emit_qk(1)
        for i, (G, n, kt, c0, c0_prev, last) in enumerate(steps):
            par = (gstep + i) % 2
            x0 = XB0[par]
            if G + 1 < NG and G == 0 and n in (0, 1):
                pj = (n, range(8), True)
            elif G + 1 < NG and G == 1 and n in (0, 1, 2, 3):
                pj = (n // 2, range(4) if n % 2 == 0 else range(4, 8), n % 2 == 1)
            elif G + 1 < NG and G >= 2 and n < 8:
                pj = (n // 4, range(2 * (n % 4), 2 * (n % 4) + 2), n % 4 == 3)
            else:
                pj = None
            op("act", lambda e: e.activation(out=SP2[:, :, c0:512], in_=E2[:, :, c0:512], func=AF.Ln, bias=1.0),
               reads=[E2_b], writes=[SP2_b])
            if i + 1 < len(steps):
                emit_E(i + 1)
            for hd in range(2):
                xb_ = x0 + hd
                op("pe", lambda e: e.matmul(ps[xb_][:, c0:512], lhsT=negU, rhs=SP2[:, hd, c0:512], start=False, stop=True,
                                            skip_group_check=True),
                   reads=[SP2_b, cst_b], writes=[psb[xb_]])
            if pend_ssq and pend_ssq[0] != (pr, G) and n < 4:
                ssq_mm(3 - n)
            if pj is not None:
                qk_proj_group(pr, G + 1, which=(pj[0],), part=(0,), kcs=pj[1])
            op("act", lambda e: e.activation(out=AT2[par][:, :, c0:512], in_=ps2(x0)[:, :, c0:512], func=AF.Exp),
               writes=[psb[x0], psb[x0 + 1], AT2_b[par]])
            for hd in range(2):
                hb_ = (pr - 4) * 2 + hd
                op("pe", lambda e: e.matmul(ps[6][hsl[hd], c0:512], lhsT=VB[:, kt, hb_ * 64:(hb_ + 1) * 64],
                                            rhs=AT2[par][:, hd, c0:512], start=(n == 0), stop=last, skip_group_check=True),
                   reads=[AT2_b[par], V_b[kt]], writes=[psb[6]])
            if i + 2 < len(steps):
                emit_qk(i + 2)
            if not last:
                cn = 512 if c0_prev is None else c0_prev
                if cn > c0:
                    op("dve", lambda e: e.tensor_copy(SPs2[:, :, c0:cn], SP2[:, :, c0:cn]), reads=[SP2_b], writes=[SPs2_b])
                if cn < 512:
                    op("dve", lambda e: e.tensor_tensor(SPs2[:, :, cn:512], SPs2[:, :, cn:512], SP2[:, :, cn:512], ALU.add),
                       reads=[SP2_b, SPs2_b], writes=[SPs2_b])
                op("dve", lambda e: e.tensor_copy(SPsb2[par][:, :, c0:512], SPs2[:, :, c0:512]), reads=[SPs2_b],
                   writes=[SPsb2_b[par]])
                xn = XB0[1 - par]
                for hd in range(2):
                    op("pe", lambda e: e.matmul(ps[xn + hd][:, c0:512], lhsT=negones, rhs=SPsb2[par][:, hd, c0:512],
                                                start=False, stop=False, skip_group_check=True),
                       reads=[SPsb2_b[par], cst_b], writes=[psb[xn + hd]])
            if pj is not None and pj[2]:
                qk_proj_group(pr, G + 1, which=(pj[0],), part=(1,))
            if last:
                op("act", lambda e: e.activation(out=sq, in_=ps[6][:, :], func=AF.Square), writes=[psb[6], sq_b])
                op("dve", lambda e: e.tensor_scalar_mul(yT[:, pr, G * 512:(G + 1) * 512], ps[6][:, :], gab[:, pr:pr + 1]),
                   reads=[cst_b], writes=[psb[6]] + yT_b[pr][4 * G:4 * G + 4])
                if G + 1 < NG:
                    pend_ssq.append((pr, G))
                else:
                    for ti in range(4):
                        ssq_mm(ti)
                    op("dve", lambda e: e.tensor_copy(ssqB[:, 4 * G:4 * G + 4, pr - 4], ps[7][:, 0:4]),
                       writes=[psb[7], ssqB_b])
            if pend_ssq and not last and n == 3 and pend_ssq[0] != (pr, G):
                ppr, pG = pend_ssq.pop(0)
                op("dve", lambda e: e.tensor_copy(ssqB[:, 4 * pG:4 * pG + 4, ppr - 4], ps[7][:, 0:4]), writes=[psb[7], ssqB_b])
        gstep += len(steps)

    sc.barrier()
    if dbg is not None:
        s_dbg = sc.dma_sem("dbg")
        dma("sp", s_dbg, dbg_d[:], yT[:].rearrange("p a s -> p (a s)"))
        dma("sp", s_dbg, dbg2_d[:, 0:NT * 8], ssqA[:].rearrange("p a s -> p (a s)"))
        dma("sp", s_dbg, dbg2_d[:, NT * 8:NT * 12], ssqB[:].rearrange("p a s -> p (a s)"))
        sc.finish([s_dbg])
    ar.release(persist_mark)
    wo = ar.alloc("wo", [128, 8, D], BF16)
    wfo = ar.alloc("wfo", [128, NJ, D], BF16)
    actT = ar.alloc("actT", [128, NJ, 512], BF16)
    h2T = ar.alloc("h2T", [128, 8, 512], BF16)
    x1 = ar.alloc("x1", [128, 5, D], F32)
    g2t = ar.alloc("g2t", [128, 8], F32)
    gfb = ar.alloc("gfb", [128, D], F32)
    wfi = [ar.alloc("wfi", [128, 2, 8, 128], BF16) for _ in range(3)]
    cw = ar.alloc("cw", [128, 2 * NJ, 3], F32)
    cb = ar.alloc("cb", [128, 2 * NJ], F32)
    halo = ar.alloc("halo", [128, 2 * NJ, 2], F32)
    htmp = ar.alloc("htmp", [128, 4], F32)
    htmp_b = Buf()
    corr = ar.alloc("corr", [128, 2 * NJ, 2], F32)
    corr_b = [Buf() for _ in range(2 * NJ)]
    cg = ar.alloc("cg", [128, 512], F32)
    cv2 = [ar.alloc("cv", [128, 512], F32) for _ in range(2)]
    sg = ar.alloc("sg", [128, 512], F32)
    junk3 = sg[:].bitcast(BF16)
    h2b2 = [ar.alloc("h2b", [128, D], BF16) for _ in range(2)]
    rA = ar.alloc("rA", [128, NT], F32)
    rB = ar.alloc("rB", [128, NT], F32)
    st3 = ar.alloc("st3", [128, 2 * NT], F32)
    w3_b = Buf(); actT_b = [Buf() for _ in range(NJ)]; h2T_b = [Buf() for _ in range(4)]
    x1_b = [Buf() for _ in range(5)]; wfi_b = [Buf(), Buf(), Buf()]; halo_b = [Buf() for _ in range(2 * NJ)]
    cg_b = Buf(); cv2_b = [Buf(), Buf()]; sg_b = Buf(); h2b2_b = [Buf(), Buf()]; r_b = Buf()
    st3_b = [Buf() for _ in range(NT)]
    s_w3 = sc.dma_sem("w3")
    s_wfo = sc.dma_sem("wfo")
    wfo_b = Buf()
    s_wfi = [sc.dma_sem(f"wfi{i}") for i in range(3)]
    s_x1 = [sc.dma_sem(f"x1_{i}") for i in range(5)]
    s_out = [sc.dma_sem(f"out{i}") for i in range(5)]
    for pr in range(8):
        dma("pool", s_w3, wo[:, pr, :], wo_d[:, pr, :], writes=[w3_b])
    dma("sp", s_w3, g2t[:], g2t_d[:], writes=[w3_b])
    dma("sp", s_w3, gfb[:], gfb_d[:], writes=[w3_b])
    dma("sp", s_w3, cw[:], cw_d[:], writes=[w3_b])
    dma("sp", s_w3, cb[:], cb_d[:], writes=[w3_b])
    op("dve", lambda e: e.tensor_reduce(out=rA[:, 0:NT], in_=ssqA[:], axis=AX.X, op=ALU.add), reads=[ssqA_b], writes=[r_b])
    op("dve", lambda e: e.tensor_reduce(out=rB[:, 0:NT], in_=ssqB[:], axis=AX.X, op=ALU.add), reads=[ssqB_b], writes=[r_b])
    rstd_ops(rA[:, 0:NT], rA[:, 0:NT], rA[:, 0:NT], 512, [r_b])
    rstd_ops(rB[:, 0:NT], rB[:, 0:NT], rB[:, 0:NT], 512, [r_b])

    def a_ops(G, ti):
        t = 4 * G + ti
        xs = t % 5
        hbuf, hbuf_b = h2b2[ti % 2], h2b2_b[ti % 2]
        dma("sp", s_x1[xs], x1[:, xs, :], x_d[t * 128:(t + 1) * 128, :], writes=[x1_b[xs]])
        for grp in range(2):
            for hf in range(2):
                bank = grp * 2 + hf
                for n_ in range(4):
                    pr = grp * 4 + n_
                    op("pe", lambda e: e.matmul(ps[bank][:, :], lhsT=yT[:, pr, t * 128:(t + 1) * 128],
                                                rhs=wo[:, pr, hf * 512:(hf + 1) * 512], start=(n_ == 0), stop=(n_ == 3)),
                       reads=[yT_b[pr][t], w3_b], writes=[psb[bank]])
        for grp in range(2):
            rr = rA if grp == 0 else rB
            for hf in range(2):
                bank = grp * 2 + hf
                op("dve", lambda e: e.scalar_tensor_tensor(out=x1[:, xs, hf * 512:(hf + 1) * 512], in0=ps[bank][:, :],
                                                           scalar=rr[:, t:t + 1],
                                                           in1=x1[:, xs, hf * 512:(hf + 1) * 512], op0=ALU.mult, op1=ALU.add),
                   reads=[r_b], writes=[psb[bank], x1_b[xs]])
        op("act", lambda e: e.activation(out=junk3, in_=x1[:, xs, :], func=AF.Square, accum_out=st3[:, 2 * t:2 * t + 1]),
           reads=[x1_b[xs]], writes=[sg_b, st3_b[t]])
        rstd_ops(st3[:, 2 * t:2 * t + 1], st3[:, 2 * t:2 * t + 1], st3[:, 2 * t:2 * t + 1], D, [st3_b[t]])
        op("dve", lambda e: e.tensor_scalar_mul(hbuf[:], x1[:, xs, :], st3[:, 2 * t:2 * t + 1]),
           reads=[x1_b[xs], st3_b[t]], writes=[hbuf_b])

    def a_tr(G, ti):
        hbuf, hbuf_b = h2b2[ti % 2], h2b2_b[ti % 2]
        pb = 6 + ti % 2
        for kc in range(8):
            op("pe", lambda e: e.transpose(out=psbf(pb)[:, kc * 128:(kc + 1) * 128], in_=hbuf[:, kc * 128:(kc + 1) * 128],
                                           identity=ident),
               reads=[hbuf_b, cst_b], writes=[psb[pb]])
        op("dve", lambda e: e.tensor_tensor(h2T[:, :, ti * 128:(ti + 1) * 128],
                                            psbf(pb)[:, 0:1024].rearrange("p (k t) -> p k t", k=8),
                                            g2t[:, :].unsqueeze(2).to_broadcast([128, 8, 128]), ALU.mult),
           reads=[w3_b], writes=[psb[pb], h2T_b[ti]])

    def wfi_load(gj):
        if gj < NG * NJ:
            sl = gj % 3
            dma("pool", s_wfi[sl], wfi[sl][:], wfi_d[gj % NJ], writes=[wfi_b[sl]])

    def b_ffn_in(G):
        if G == 0:
            wfi_load(0)
            wfi_load(1)
        for j in range(NJ):
            gj = G * NJ + j
            sl = gj % 3
            wfi_load(gj + 2)
            if G == 0:
                dma("pool", s_wfo, wfo[:, j, :], wfo_d[:, j, :], writes=[wfo_b])
            banks = (2, 3) if j % 2 == 0 else (0, 1)
            for gv in range(2):
                for kc in range(8):
                    op("pe", lambda e: e.matmul(ps[banks[gv]][:, :], lhsT=wfi[sl][:, gv, kc, :], rhs=h2T[:, kc, :],
                                                start=(kc == 0), stop=(kc == 7)),
                       reads=[wfi_b[sl]] + h2T_b, writes=[psb[banks[gv]]])
            for gv in range(2):
                idx = gv * NJ + j
                cx, cx_b = (cg, cg_b) if gv == 0 else (cv2[j % 2], cv2_b[j % 2])
                u = ps[banks[gv]]
                ub = psb[banks[gv]]
                op("act", lambda e: e.activation(out=cx[:], in_=u[:, :], func=AF.Identity, scale=cw[:, idx, 2:3],
                                                 bias=cb[:, idx:idx + 1]),
                   reads=[w3_b], writes=[ub, cx_b])
                if G > 0:
                    op("pool", lambda e: e.tensor_tensor(cx[:, 0:2], cx[:, 0:2], corr[:, idx, :], ALU.add),
                       reads=[corr_b[idx]], writes=[cx_b])
                if G + 1 < NG:
                    op("act", lambda e: e.activation(out=halo[:, idx, :], in_=u[:, 510:512], func=AF.Copy),
                       writes=[ub, halo_b[idx]])
                op("dve", lambda e: e.scalar_tensor_tensor(out=cx[:, 1:512], in0=u[:, 0:511], scalar=cw[:, idx, 1:2],
                                                           in1=cx[:, 1:512], op0=ALU.mult, op1=ALU.add),
                   reads=[w3_b], writes=[ub, cx_b])
                op("dve", lambda e: e.scalar_tensor_tensor(out=cx[:, 2:512], in0=u[:, 0:510], scalar=cw[:, idx, 0:1],
                                                           in1=cx[:, 2:512], op0=ALU.mult, op1=ALU.add),
                   reads=[w3_b], writes=[ub, cx_b])
            op("act", lambda e: e.activation(out=sg[:], in_=cg[:], func=AF.Silu), reads=[cg_b], writes=[sg_b])
            op("dve", lambda e: e.tensor_tensor(actT[:, j, :], sg[:], cv2[j % 2][:], ALU.mult), reads=[sg_b, cv2_b[j % 2]],
               writes=[actT_b[j]])
            if G + 1 < NG:
                for gv in range(2):
                    idx = gv * NJ + j
                    op("pool", lambda e: e.tensor_scalar_mul(corr[:, idx, 0:2], halo[:, idx, 0:2], cw[:, idx, 0:1]),
                       reads=[w3_b, halo_b[idx]], writes=[corr_b[idx]])
                    op("pool", lambda e: e.tensor_scalar_mul(htmp[:, 0:1], halo[:, idx, 1:2], cw[:, idx, 1:2]),
                       reads=[w3_b, halo_b[idx]], writes=[htmp_b])
                    op("pool", lambda e: e.tensor_tensor(corr[:, idx, 0:1], corr[:, idx, 0:1], htmp[:, 0:1], ALU.add),
                       reads=[htmp_b], writes=[corr_b[idx]])

    def c_ffn_out(G, ti):
        t = 4 * G + ti
        xs = t % 5
        for hf in range(2):
            bank = 4 + hf
            for j in range(NJ):
                op("pe", lambda e: e.matmul(ps[bank][:, :], lhsT=actT[:, j, ti * 128:(ti + 1) * 128],
                                            rhs=wfo[:, j, hf * 512:(hf + 1) * 512], start=(j == 0), stop=(j == NJ - 1)),
                   reads=[actT_b[j], wfo_b], writes=[psb[bank]])
            op("dve", lambda e: e.tensor_tensor(x1[:, xs, hf * 512:(hf + 1) * 512], ps[bank][:, :],
                                                x1[:, xs, hf * 512:(hf + 1) * 512], ALU.add),
               writes=[psb[bank], x1_b[xs]])
        op("act", lambda e: e.activation(out=junk3, in_=x1[:, xs, :], func=AF.Square, accum_out=st3[:, 2 * t + 1:2 * t + 2]),
           reads=[x1_b[xs]], writes=[sg_b, st3_b[t]])
        rstd_ops(st3[:, 2 * t + 1:2 * t + 2], st3[:, 2 * t + 1:2 * t + 2], st3[:, 2 * t + 1:2 * t + 2], D, [st3_b[t]])
        op("dve", lambda e: e.scalar_tensor_tensor(out=x1[:, xs, :], in0=x1[:, xs, :], scalar=st3[:, 2 * t + 1:2 * t + 2],
                                                   in1=gfb[:], op0=ALU.mult, op1=ALU.mult),
           reads=[st3_b[t], w3_b], writes=[x1_b[xs]])
        dma("sp", s_out[xs], out_d[t * 128:(t + 1) * 128, :], x1[:, xs, :], reads=[x1_b[xs]])

    a_ops(0, 0)
    for ti in range(1, 4):
        a_ops(0, ti)
        a_tr(0, ti - 1)
    a_tr(0, 3)
    for G in range(NG):
        b_ffn_in(G)
        nxt = G + 1 < NG
        if nxt:
            a_ops(G + 1, 0)
        for ti in range(4):
            c_ffn_out(G, ti)
            if nxt:
                a_tr(G + 1, ti)
                if ti + 1 < 4:
                    a_ops(G + 1, ti + 1)
    sc.finish(s_out)
    build_nc.stats = (sc.n_inst, sc.n_wait)
    return nc


def host_inputs(x, norm1_g, w_in, rel_bias, norm_a_g, norm_b_g, w_out, norm2_g, w_ffn_in, conv_w, conv_b,
                w_ffn_out, final_g):
    f = np.float32
    w_in = np.asarray(w_in, f)[0]
    wk = w_in.reshape(8, 128, 3072)
    qbase = [pr * 128 for pr in range(4)] + [1536 + pr * 128 for pr in range(4)]
    kbase = [512 + pr * 128 for pr in range(4)] + [2048 + pr * 128 for pr in range(4)]
    wqk = np.empty((8, 128, 2, 8, 128), f)
    for pr in range(8):
        wqk[pr, :, 0] = wk[:, :, qbase[pr]:qbase[pr] + 128].transpose(1, 0, 2)
        wqk[pr, :, 1] = wk[:, :, kbase[pr]:kbase[pr] + 128].transpose(1, 0, 2)
    wv = np.empty((2, 128, 8, 512), f)
    wv[0] = wk[:, :, 1024:1536].transpose(1, 0, 2)
    wv[1] = wk[:, :, 2560:3072].transpose(1, 0, 2)
    bc = lambda v: np.ascontiguousarray(np.broadcast_to(np.asarray(v, f).reshape(1, -1), (128, np.asarray(v).size)))
    gfb = bc(final_g)
    g1t = np.ascontiguousarray(np.asarray(norm1_g, f)[0].reshape(8, 128).T)
    g2t = np.ascontiguousarray(np.asarray(norm2_g, f)[0].reshape(8, 128).T)
    rb = np.asarray(rel_bias, f)[0]
    k = np.arange(128)[:, None]; q = np.arange(128)[None, :]
    biasT = np.empty((8, 128, 2, 128), f)
    for jj, j in enumerate((3, 4)):
        idx = np.clip((4 - j) * 128 + q - k, -128, 128) + 128
        biasT[:, :, jj, :] = rb[:, idx]
    c256 = bc(rb[:, 256])
    gcat = np.concatenate([np.asarray(norm_a_g, f)[0], np.asarray(norm_b_g, f)[0]])
    gab = np.ascontiguousarray(gcat.reshape(8, 128).T)
    wo = np.ascontiguousarray(np.asarray(w_out, f)[0].reshape(8, 128, D).transpose(1, 0, 2))
    wfi_full = np.asarray(w_ffn_in, f)[0].reshape(8, 128, 2, NJ, 128)
    wfi = np.ascontiguousarray(wfi_full.transpose(3, 1, 2, 0, 4))
    cwf = np.asarray(conv_w, f)[0].reshape(3, 2 * NJ, 128)
    cw = np.ascontiguousarray(cwf.transpose(2, 1, 0))
    cb = np.ascontiguousarray(np.asarray(conv_b, f)[0].reshape(2 * NJ, 128).T)
    wfo = np.ascontiguousarray(np.asarray(w_ffn_out, f)[0].reshape(NJ, 128, D).transpose(1, 0, 2))
    cst = np.zeros((128, 6, 128), f)
    cst[:, 0, :] = np.eye(128, dtype=f)
    cst[:, 1, :] = -(k >= q).astype(f)
    cst[:, 2, :] = -1.0
    cst[:, 3, :] = np.where(k >= q, -30000.0, 0.0)
    cst[:, 4, :] = np.where((k < 64) & (q >= 64), NEG, 0.0)
    cst[:, 5, :] = np.where((k >= 64) & (q < 64), NEG, 0.0)
    shared = dict(wqk=wqk, wv=wv, g1t=g1t, g2t=g2t, gfb=gfb, biasT=biasT, c256=c256, gab=gab, wo=wo, wfi=wfi,
                  cw=cw, cb=cb, wfo=wfo, cst=cst)
    return shared


_NC_CACHE = {}


def kernel(x, norm1_g, w_in, rel_bias, norm_a_g, norm_b_g, w_out, norm2_g, w_ffn_in, conv_w, conv_b,
           w_ffn_out, final_g):
    x = np.asarray(x, np.float32)
    B, S, _ = x.shape
    shared = host_inputs(x, norm1_g, w_in, rel_bias, norm_a_g, norm_b_g, w_out, norm2_g, w_ffn_in, conv_w,
                         conv_b, w_ffn_out, final_g)
    if S not in _NC_CACHE:
        _NC_CACHE[S] = build_nc(S)
    nc = _NC_CACHE[S]
    in_maps = [dict(shared, x=np.ascontiguousarray(x[b])) for b in range(B)]
    res = run_bass_kernel_spmd(nc, in_maps, core_ids=list(range(B)))
    return np.stack([np.asarray(r["out"], np.float32) for r in res.results], axis=0)
```
